# Optimizing a Trainium2 kernel written in Bass

```python
import math
import jax
import jax.numpy as jnp
from jax import lax
import numpy as np

D_MODEL = 4096
BATCH = 4
SEQ = 2048
DEPTH = 2

HG_HEADS = 8
HG_DK = 128
HG_DV = 128
HG_WIDTH = HG_HEADS * HG_DV
HG_CHUNK = 64
POOL_WINDOWS = (2, 4, 8, 16)
POOL_GROUPS = 4
POOL_GROUP_DIM = 256
POOL_WIDTH = POOL_GROUPS * POOL_GROUP_DIM
DA_HEADS = 8
DA_HEAD_DIM = 64
DA_VDIM = 2 * DA_HEAD_DIM
DA_WIDTH = DA_HEADS * DA_VDIM
Q_BLOCK = 128
N_BRANCH = 3
FFN_HIDDEN = -(-8 * D_MODEL // (3 * 256)) * 256
NORM_EPS = 1e-6

IN_SPLITS = (
    HG_HEADS * HG_DK,
    HG_HEADS * HG_DK,
    HG_WIDTH,
    HG_WIDTH,
    POOL_WIDTH,
    DA_HEADS * 2 * DA_HEAD_DIM,
    DA_HEADS * 2 * DA_HEAD_DIM,
    DA_WIDTH,
    N_BRANCH * D_MODEL,
)
IN_COLS = sum(IN_SPLITS)

kernel_name = "hybrid_hgrn2_pool_diffattn_gated_block"


def rms_norm(x, gain):
    xf = x.astype(jnp.float32)
    y = xf * lax.rsqrt(jnp.mean(xf * xf, axis=-1, keepdims=True) + NORM_EPS)
    return (y * gain.astype(jnp.float32)).astype(x.dtype)


def split_columns(proj):
    pieces, start = [], 0
    for size in IN_SPLITS:
        pieces.append(proj[..., start:start + size])
        start += size
    return pieces


def alibi_slopes(n_heads):
    return 2.0 ** (-8.0 * jnp.arange(1, n_heads + 1, dtype=jnp.float32) / n_heads)


def hgrn2_mixer(q, f_raw, v, g, lb, out_gain):
    B, S, _ = q.shape
    nc = S // HG_CHUNK
    f = lb.astype(jnp.float32) + (1.0 - lb.astype(jnp.float32)) * jax.nn.sigmoid(f_raw.astype(jnp.float32))
    log_f = jnp.log(f)
    k = 1.0 - f

    def to_chunks(t, d):
        return t.astype(jnp.float32).reshape(B, nc, HG_CHUNK, HG_HEADS, d).transpose(1, 0, 3, 2, 4)

    qc = to_chunks(q, HG_DK) * (HG_DK ** -0.5)
    kc = to_chunks(k, HG_DK)
    gc = to_chunks(log_f, HG_DK)
    vc = to_chunks(v, HG_DV)
    causal = jnp.tril(jnp.ones((HG_CHUNK, HG_CHUNK), dtype=bool))

    def chunk_step(state, inp):
        qb, kb, vb, gb = inp
        b = jnp.cumsum(gb, axis=2)
        o_inter = jnp.einsum('bhtk,bhkv->bhtv', qb * jnp.exp(b), state)
        diff = b[:, :, :, None, :] - b[:, :, None, :, :]
        decay = jnp.exp(jnp.where(causal[:, :, None], diff, -jnp.inf))
        scores = jnp.sum(qb[:, :, :, None, :] * kb[:, :, None, :, :] * decay, axis=-1)
        o_intra = jnp.einsum('bhts,bhsv->bhtv', scores, vb)
        b_last = b[:, :, -1:, :]
        new_state = jnp.exp(b_last[:, :, 0, :])[..., None] * state + jnp.einsum(
            'bhsk,bhsv->bhkv', kb * jnp.exp(b_last - b), vb)
        return new_state, o_inter + o_intra

    state0 = jnp.zeros((B, HG_HEADS, HG_DK, HG_DV), jnp.float32)
    _, o = lax.scan(chunk_step, state0, (qc, kc, vc, gc))
    o = o.transpose(1, 0, 3, 2, 4).reshape(B, S, HG_HEADS, HG_DV)
    o = o * lax.rsqrt(jnp.mean(o * o, axis=-1, keepdims=True) + NORM_EPS)
    o = o.reshape(B, S, HG_WIDTH) * out_gain.astype(jnp.float32)
    return (o * jax.nn.silu(g.astype(jnp.float32))).astype(q.dtype)


def pool_mixer(u, w_groups, scale):
    B, S, _ = u.shape
    uf = u.astype(jnp.float32).reshape(B, S, POOL_GROUPS, POOL_GROUP_DIM)
    cs = jnp.concatenate([jnp.zeros((B, 1, POOL_GROUPS, POOL_GROUP_DIM), jnp.float32),
                          jnp.cumsum(uf, axis=1)], axis=1)
    t = jnp.arange(S)
    outs = []
    for j, w in enumerate(POOL_WINDOWS):
        start = jnp.maximum(t + 1 - w, 0)
        win_sum = cs[:, t + 1, j] - cs[:, start, j]
        count = (t + 1 - start).astype(jnp.float32)[None, :, None]
        outs.append(win_sum / count - uf[:, :, j])
    pooled = jnp.stack(outs, axis=2)
    mixed = jnp.einsum('bsgc,gcd->bsgd', pooled, w_groups.astype(jnp.float32))
    return (mixed.reshape(B, S, POOL_WIDTH) * scale.astype(jnp.float32)).astype(u.dtype)


def diff_attention(q, k, v, lam_params, subln_gain, lambda_init):
    B, S, _ = q.shape
    nb = S // Q_BLOCK
    qf = q.astype(jnp.float32).reshape(B, S, DA_HEADS, 2, DA_HEAD_DIM) * (DA_HEAD_DIM ** -0.5)
    kf = k.astype(jnp.float32).reshape(B, S, DA_HEADS, 2, DA_HEAD_DIM)
    vf = v.astype(jnp.float32).reshape(B, S, DA_HEADS, DA_VDIM)
    lp = lam_params.astype(jnp.float32)
    lam = jnp.exp(jnp.sum(lp[0] * lp[1])) - jnp.exp(jnp.sum(lp[2] * lp[3])) + lambda_init
    slopes = alibi_slopes(DA_HEADS)
    key_pos = jnp.arange(S)
    q_blocks = qf.reshape(B, nb, Q_BLOCK, DA_HEADS, 2, DA_HEAD_DIM).transpose(1, 0, 2, 3, 4, 5)

    def one_block(args):
        q_blk, blk = args
        q_pos = blk * Q_BLOCK + jnp.arange(Q_BLOCK)
        dist = (q_pos[:, None] - key_pos[None, :]).astype(jnp.float32)
        bias = -slopes[:, None, None] * dist
        s = jnp.einsum('bqhmd,bkhmd->bhmqk', q_blk, kf) + bias[None, :, None]
        s = jnp.where(dist >= 0, s, -jnp.inf)
        p = jax.nn.softmax(s, axis=-1)
        a = p[:, :, 0] - lam * p[:, :, 1]
        return jnp.einsum('bhqk,bkhv->bqhv', a, vf)

    o = lax.map(one_block, (q_blocks, jnp.arange(nb)))
    o = o.transpose(1, 0, 2, 3, 4).reshape(B, S, DA_HEADS, DA_VDIM)
    o = o * lax.rsqrt(jnp.mean(o * o, axis=-1, keepdims=True) + NORM_EPS) * subln_gain.astype(jnp.float32)
    o = o * (1.0 - lambda_init)
    return o.reshape(B, S, DA_WIDTH).astype(q.dtype)


def setup_inputs(seed: int = 0) -> dict:
    key = jax.random.key(seed)
    ks = jax.random.split(key, 20)
    L, D, F = DEPTH, D_MODEL, FFN_HIDDEN

    def nrm(k, shape, scale):
        return jax.random.normal(k, shape, jnp.float32) * scale

    def gain(k, shape):
        return 1.0 + 0.02 * jax.random.normal(k, shape, jnp.float32)

    return {
        "x": nrm(ks[0], (BATCH, SEQ, D), 1.0),
        "norm_mix_pre": gain(ks[1], (L, D)),
        "norm_mix_post": gain(ks[2], (L, D)),
        "norm_ffn_pre": gain(ks[3], (L, D)),
        "norm_ffn_post": gain(ks[4], (L, D)),
        "w_in": nrm(ks[5], (L, D, IN_COLS), D ** -0.5),
        "hgrn_lb_logits": nrm(ks[6], (L, HG_HEADS * HG_DK), 0.5),
        "hgrn_out_norm": gain(ks[7], (L, HG_WIDTH)),
        "pool_w": nrm(ks[8], (L, POOL_GROUPS, POOL_GROUP_DIM, POOL_GROUP_DIM), POOL_GROUP_DIM ** -0.5),
        "pool_scale": gain(ks[9], (L, POOL_WIDTH)),
        "diff_lambda": nrm(ks[10], (L, 4, DA_HEAD_DIM), 0.1),
        "diff_subln": gain(ks[11], (L, DA_VDIM)),
        "w_up_a": nrm(ks[12], (L, HG_WIDTH, D), HG_WIDTH ** -0.5),
        "w_up_b": nrm(ks[13], (L, POOL_WIDTH, D), POOL_WIDTH ** -0.5),
        "w_up_c": nrm(ks[14], (L, DA_WIDTH, D), DA_WIDTH ** -0.5),
        "w_out": nrm(ks[15], (L, D, D), D ** -0.5),
        "w_ffn_gate": nrm(ks[16], (L, D, F), D ** -0.5),
        "w_ffn_up": nrm(ks[17], (L, D, F), D ** -0.5),
        "w_ffn_down": nrm(ks[18], (L, F, D), F ** -0.5),
    }


def reference(x, norm_mix_pre, norm_mix_post, norm_ffn_pre, norm_ffn_post, w_in,
              hgrn_lb_logits, hgrn_out_norm, pool_w, pool_scale, diff_lambda, diff_subln,
              w_up_a, w_up_b, w_up_c, w_out, w_ffn_gate, w_ffn_up, w_ffn_down):
    B, S, D = x.shape
    lb_all = jnp.cumsum(jax.nn.softmax(hgrn_lb_logits.astype(jnp.float32), axis=0), axis=0)
    lb_all = lb_all - lb_all[0:1]
    for l in range(DEPTH):
        lambda_init = 0.8 - 0.6 * math.exp(-0.3 * l)
        h = rms_norm(x, norm_mix_pre[l])
        proj = h @ w_in[l]
        hq, hf, hv, hg, pu, dq, dk, dv, gate_logits = split_columns(proj)
        y_a = hgrn2_mixer(hq, hf, hv, hg, lb_all[l], hgrn_out_norm[l])
        y_b = pool_mixer(pu, pool_w[l], pool_scale[l])
        y_c = diff_attention(dq, dk, dv, diff_lambda[l], diff_subln[l], lambda_init)
        gates = jax.nn.sigmoid(gate_logits.astype(jnp.float32)).reshape(B, S, N_BRANCH, D)
        merged = (gates[:, :, 0] * (y_a @ w_up_a[l])
                  + gates[:, :, 1] * (y_b @ w_up_b[l])
                  + gates[:, :, 2] * (y_c @ w_up_c[l])).astype(x.dtype)
        x = x + rms_norm(merged @ w_out[l], norm_mix_post[l])
        h = rms_norm(x, norm_ffn_pre[l])
        ff = (jax.nn.silu(h @ w_ffn_gate[l]) * (h @ w_ffn_up[l])) @ w_ffn_down[l]
        x = x + rms_norm(ff, norm_ffn_post[l])
    return x
```

```python
import math
import numpy as np
import ml_dtypes
import concourse.bass as bass
import concourse.mybir as mybir
from concourse.bass_utils import run_bass_kernel_spmd

F32 = mybir.dt.float32
BF16 = mybir.dt.bfloat16
ALU = mybir.AluOpType
AF = mybir.ActivationFunctionType
AX = mybir.AxisListType
NP_BF16 = ml_dtypes.bfloat16
EPS = 1e-6
ENG = ["pe", "act", "dve", "pool", "sp"]


class Cfg:
    def __init__(s, D=4096, S=2048, B=4, HGH=8, NG=4, DAH=8, F=11008, depth=2, gather=True):
        s.D, s.S, s.B, s.HGH, s.NG, s.DAH, s.F, s.depth = D, S, B, HGH, NG, DAH, F, depth
        s.gather = gather
        s.T = S // 2
        s.KD = D // 128
        s.HW_ = HGH * 128
        s.PW = NG * 256
        s.AW = DAH * 128
        s.hh = HGH // 2
        s.gh = NG // 2
        s.ah = DAH // 2
        s.NY = (s.HW_ + s.PW + s.AW) // 128
        s.NYH = s.NY // 2
        s.FB = F // 128
        s.in_splits = [s.HW_, s.HW_, s.HW_, s.HW_, s.PW, s.AW, s.AW, s.AW, 3 * D]
        s.in_off = np.concatenate([[0], np.cumsum(s.in_splits)]).tolist()


class Prog:
    def __init__(s, nc):
        s.nc = nc
        s.q = {e: [] for e in ENG}
        s.cnt = {e: 0 for e in ENG}
        s.nops = {e: 0 for e in ENG}
        s.waited = {e: {} for e in ENG}
        s.dcnt = []
        s.last = {e: None for e in ENG}

    def new_dsem(s):
        s.dcnt.append(0)
        return len(s.dcnt) - 1

    def _waits(s, eng, deps):
        waits = []
        for d in deps:
            if d is None:
                continue
            if d[0] == "e":
                _, e2, n, idx = d
                if e2 == eng and eng == "pe":
                    continue
                key = ("e", e2)
            else:
                key = ("d", d[1])
                n = d[2]
            if s.waited[eng].get(key, 0) >= n:
                continue
            s.waited[eng][key] = n
            waits.append((key, n))
        return waits

    def op(s, eng, fn, deps=(), sig=True):
        waits = s._waits(eng, deps)
        tok = None
        if sig:
            s.cnt[eng] += 1
            tok = ("e", eng, s.cnt[eng], s.nops[eng])
            s.last[eng] = tok
        s.q[eng].append((fn, waits, ("e", eng) if sig else None))
        s.nops[eng] += 1
        return tok

    def dma(s, eng, fn, dsem, deps=()):
        waits = s._waits(eng, deps)
        s.dcnt[dsem] += 16
        s.q[eng].append((fn, waits, ("d", dsem)))
        s.nops[eng] += 1
        return ("d", dsem, s.dcnt[dsem])

    def coll(s, fn, csem, deps=()):
        waits = s._waits("pool", deps)
        s.dcnt[csem] += 1
        s.q["pool"].append((fn, waits, ("c", csem)))
        s.nops["pool"] += 1
        return ("d", csem, s.dcnt[csem])

    def barrier(s, engines=("pe", "act", "dve", "pool", "sp"), extra=()):
        toks = [s.last[e] for e in ENG if s.last[e] is not None] + list(extra)
        for e in engines:
            s.op(e, None, deps=toks, sig=False)

    def emit(s, final_waits):
        nc = s.nc
        import contextlib
        with contextlib.ExitStack() as st:
            esem = {e: st.enter_context(nc.semaphore("p_" + e)) for e in ENG}
            dsem = [st.enter_context(nc.semaphore("d%d" % i)) for i in range(len(s.dcnt))]
            block = st.enter_context(nc.Block())

            def sem_of(key):
                return esem[key[1]] if key[0] == "e" else dsem[key[1]]

            def run(engname, eng):
                for fn, waits, sig in s.q[engname]:
                    for key, n in waits:
                        eng.wait_ge(sem_of(key), n)
                    if fn is None:
                        continue
                    ins = fn(eng)
                    if sig is not None:
                        if sig[0] == "e":
                            ins.then_inc(esem[sig[1]], 1)
                        elif sig[0] == "d":
                            ins.then_inc(dsem[sig[1]], 16)
                        else:
                            ins.then_inc(dsem[sig[1]])
                if engname == "sp":
                    for tok in final_waits:
                        if tok[0] == "e":
                            eng.wait_ge(esem[tok[1]], tok[2])
                        else:
                            eng.wait_ge(dsem[tok[1]], tok[2])

            @block.tensor
            def _(e):
                run("pe", e)

            @block.scalar
            def _(e):
                run("act", e)

            @block.vector
            def _(e):
                run("dve", e)

            @block.gpsimd
            def _(e):
                run("pool", e)

            @block.sync
            def _(e):
                run("sp", e)


class Slot:
    def __init__(s, ap, dsem):
        s.ap, s.dsem = ap, dsem
        s.users = []
        s.fill = None


class Ring:
    def __init__(s, P, aps):
        s.slots = [Slot(a, P.new_dsem()) for a in aps]
        s.i = 0

    def next(s):
        sl = s.slots[s.i % len(s.slots)]
        s.i += 1
        return sl


CH_ELEMS = 4 * 1024 * 1024
TEN_ELEMS = 8 * CH_ELEMS


class WStream:
    def __init__(s, P, nc, cfg, n_cores=8):
        s.P, s.nc, s.cfg = P, nc, cfg
        s.plan = []
        s.off = 0
        s.n_cores = n_cores
        s.full = None
        s.csem = P.new_dsem() if cfg.gather else None

    def block(s, key, shape, ring, extra_deps=()):
        n = int(np.prod(shape))
        if s.cfg.gather and (s.off % TEN_ELEMS) + n > TEN_ELEMS:
            padn = TEN_ELEMS - (s.off % TEN_ELEMS)
            s.plan.append((("pad",), padn))
            s.off += padn
        off = s.off
        s.plan.append((key, n))
        s.off += n
        sl = ring.next()
        per = n // shape[0]
        holder = s

        def fn(eng, off=off, per=per, sl=sl, shape=shape):
            if holder.cfg.gather:
                full = holder.fulls[off // TEN_ELEMS]
                o2 = off % TEN_ELEMS
            else:
                full, o2 = holder.full, off
            src = full[o2:o2 + shape[0] * per].rearrange("(p f) -> p f", p=shape[0])
            if len(shape) == 3:
                src = src.rearrange("p (a b) -> p a b", a=shape[1])
                dst = sl.ap[:, 0:shape[1], 0:shape[2]]
            else:
                dst = sl.ap[:, 0:shape[1]]
            return eng.dma_start(out=dst, in_=src)

        deps = list(sl.users) + list(extra_deps)
        if s.cfg.gather:
            deps.append(("d", s.csem, (off + n - 1) // CH_ELEMS + 1))
        sl.fill = s.P.dma("pool", fn, sl.dsem, deps)
        sl.users = []
        return sl

    def total_padded(s):
        unit = CH_ELEMS if s.cfg.gather else 128
        return max(unit, ((s.off + unit - 1) // unit) * unit)

    def finalize(s):
        nc, P = s.nc, s.P
        tot = s.total_padded()
        if not s.cfg.gather:
            s.full = nc.dram_tensor("wstream", [tot], F32, kind="ExternalInput").ap()
            return
        nch = tot // CH_ELEMS
        piece = CH_ELEMS // s.n_cores
        wsh = nc.dram_tensor("wstream", [nch, piece], F32, kind="ExternalInput").ap()
        wb_t = nc.dram_tensor("wbounce", [nch, piece], F32)
        nten = (nch + 7) // 8
        wf_ts = [nc.dram_tensor("wfull%d" % i, [min(8, nch - 8 * i), CH_ELEMS], F32) for i in range(nten)]
        s.fulls = [t.ap().rearrange("a b -> (a b)") for t in wf_ts]
        bsem = P.new_dsem()
        front = []
        grp = list(range(s.n_cores))
        for c in range(nch):
            front.append((lambda eng, c=c: eng.dma_start(out=wb_t.ap()[c:c + 1, :], in_=wsh[c:c + 1, :]),
                          [], ("d", bsem)))
            front.append((lambda eng, c=c: eng.collective_compute(
                "AllGather", ALU.bypass, replica_groups=[grp],
                ins=[wb_t.ap()[c:c + 1, :].opt()], outs=[wf_ts[c // 8].ap()[c % 8:c % 8 + 1, :].opt()]),
                [(("d", bsem), 16 * (c + 1))], ("c", s.csem)))
        P.q["pool"] = front + P.q["pool"]
        P.dcnt[bsem] = 16 * nch
        P.dcnt[s.csem] = nch


def pack_stream(plan, total, getter):
    buf = np.zeros(total, np.float32)
    off = 0
    for key, n in plan:
        if key[0] == "pad":
            off += n
            continue
        blk = getter(key)
        assert blk.size == n, (key, blk.shape, n)
        buf[off:off + n] = np.ascontiguousarray(blk, dtype=np.float32).reshape(-1)
        off += n
    return buf


class Ctx:
    def __init__(s, nc, cfg, st):
        s.nc, s.cfg, s.st = nc, cfg, st
        s.P = Prog(nc)
        s.ws = WStream(s.P, nc, cfg)
        s.ps = st.enter_context(nc.psum_tensor("ps", [128, 8, 512], F32))
        s.psb = s.ps[:].bitcast(BF16)
        s.ps_users = [[] for _ in range(8)]
        s.ident = st.enter_context(nc.sbuf_tensor("ident", [128, 128], BF16))
        s.identf = st.enter_context(nc.sbuf_tensor("identf", [128, 128], F32))
        s.init_toks = []
        P = s.P
        t0 = P.op("pool", lambda e: e.memset(s.identf[:], 0.0))
        t1 = P.op("pool", lambda e: e.affine_select(
            out=s.identf[:], in_=s.identf[:], pattern=[[-1, 128]], compare_op=ALU.not_equal,
            fill=1.0, base=0, channel_multiplier=1), deps=[t0])
        t2 = P.op("dve", lambda e: e.tensor_copy(out=s.ident[:], in_=s.identf[:]), deps=[t1])
        s.epsc = st.enter_context(nc.sbuf_tensor("epsc", [128, 2], F32))
        t3 = P.op("pool", lambda e: e.memset(s.epsc[:], EPS))
        s.init_toks = [t1, t2, t3]
        P.barrier()

    def sb(s, name, shape, dt):
        return s.st.enter_context(s.nc.sbuf_tensor(name, shape, dt))


def bcast_load(C, eng, dst, src_vec_ap, dsem, deps=()):
    return C.P.dma(eng, lambda e: e.dma_start(out=dst, in_=src_vec_ap.partition_broadcast(128)), dsem, deps)


def emit_rstd(C, src, tmp, dst, scale, deps):
    P = C.P
    t_a = P.op("act", lambda e: e.activation(out=tmp, in_=src, func=AF.Sqrt, bias=C.epsc[:, 0:1], scale=scale), deps=deps)
    return P.op("dve", lambda e: e.reciprocal(out=dst, in_=tmp), deps=[t_a])


def emit_norm_transpose(C, x_src, n_tiles, gain_sl, hT, xring, hbring, ssb, extra_first=(), tok_offset=0,
                        pre=None):
    P, cfg = C.P, C.cfg
    D, KD = cfg.D, cfg.KD
    out_toks = []
    for tt in range(n_tiles):
        if pre is None:
            xs = xring.next()
            xs.fill = P.dma("sp", lambda e, xs=xs, tt=tt: e.dma_start(
                out=xs.ap, in_=x_src[tt * 128:(tt + 1) * 128, :]), xs.dsem, deps=xs.users + list(extra_first))
            xs.users = []
        else:
            xs = pre(tt)
        hb = hbring.next()
        col = ssb[:, 4 * (tt % 8):4 * (tt % 8) + 4]
        t_sq = P.op("act", lambda e, xs=xs, hb=hb, col=col: e.activation(
            out=hb.ap, in_=xs.ap, func=AF.Square, accum_out=col[:, 0:1]), deps=[xs.fill] + hb.users)
        t_r2 = emit_rstd(C, col[:, 0:1], col[:, 1:2], col[:, 2:3], 1.0 / D, [t_sq])
        t_h = P.op("dve", lambda e, xs=xs, hb=hb, col=col: e.scalar_tensor_tensor(
            out=hb.ap, in0=xs.ap, scalar=col[:, 2:3], in1=gain_sl.ap, op0=ALU.mult, op1=ALU.mult),
            deps=[t_r2, t_sq, gain_sl.fill])
        xs.users.append(t_h)
        xs.users.append(t_sq)
        for g0 in range(0, KD, 8):
            gn = min(8, KD - g0)
            bank = 6 + (g0 // 8) % 2
            deps = [t_h] + C.ps_users[bank] + C.init_toks
            tl = None
            for j in range(gn):
                kc = g0 + j
                tl = P.op("pe", lambda e, bank=bank, j=j, kc=kc, hb=hb: e.transpose(
                    out=C.psb[:, bank, j * 128:(j + 1) * 128], in_=hb.ap[:, kc * 128:(kc + 1) * 128],
                    identity=C.ident[:]), deps=deps if j == 0 else (), sig=(j == gn - 1))
            eng = "act" if (g0 // 8) % 2 == 0 else "dve"
            t0 = tok_offset + tt * 128

            def cp(e, bank=bank, g0=g0, gn=gn, t0=t0, eng=eng):
                src = C.psb[:, bank, 0:gn * 128].rearrange("p (a b) -> p a b", a=gn)
                dst = hT[:, g0:g0 + gn, t0:t0 + 128]
                if eng == "act":
                    return e.copy(out=dst, in_=src)
                return e.tensor_copy(out=dst, in_=src)
            tc_ = P.op(eng, cp, deps=[tl])
            C.ps_users[bank] = [tc_]
            out_toks.append(tc_)
        hb.users = [tl]
    return out_toks


def emit_proj_feat(C, key, KC, rhs_fn, rhs_deps, n_tok, wring, evac, banks):
    P = C.P
    wsl = C.ws.block(key, [128, KC, 128], wring)
    nb = (n_tok + 511) // 512
    deps = [wsl.fill] + list(rhs_deps)
    for b in banks[:nb]:
        deps += C.ps_users[b]
    last = None
    first = True
    for bi in range(nb):
        n = min(512, n_tok - bi * 512)
        for kc in range(KC):
            last = P.op("pe", lambda e, bi=bi, kc=kc, n=n, wsl=wsl: e.matmul(
                C.ps[:, banks[bi], 0:n], lhsT=wsl.ap[:, kc, :], rhs=rhs_fn(kc, bi * 512, n),
                start=(kc == 0), stop=(kc == KC - 1)), deps=deps if first else (),
                sig=(bi == nb - 1 and kc == KC - 1))
            first = False
    wsl.users.append(last)
    aps = [C.ps[:, banks[bi], 0:min(512, n_tok - bi * 512)] for bi in range(nb)]
    toks = evac(aps, last)
    for b in banks[:nb]:
        C.ps_users[b] = list(toks)
    return last


def emit_proj_tok(C, key_fn, kgroups, lhs_fn, lhs_deps_fn, n_tt, ncols, wring, evac):
    P = C.P
    deps0 = []
    for b in range(n_tt):
        deps0 += C.ps_users[b]
    last = None
    kidx = 0
    nk = sum(kgroups)
    for g, gs in enumerate(kgroups):
        wsl = C.ws.block(key_fn(g), [128, gs, ncols], wring)
        ldeps, reg, lsl = lhs_deps_fn(g)
        deps = [wsl.fill] + list(ldeps) + (deps0 if g == 0 else [])
        first = True
        for tt in range(n_tt):
            for j in range(gs):
                k = kidx + j
                last = P.op("pe", lambda e, tt=tt, j=j, g=g, k=k, wsl=wsl, lsl=lsl: e.matmul(
                    C.ps[:, tt, 0:ncols], lhsT=lhs_fn(lsl, j, tt), rhs=wsl.ap[:, j, 0:ncols],
                    start=(k == 0), stop=(k == nk - 1)), deps=deps if first else (),
                    sig=(tt == n_tt - 1 and j == gs - 1))
                first = False
        kidx += gs
        wsl.users.append(last)
        reg(last)
    toks = evac(last)
    for b in range(n_tt):
        C.ps_users[b] = list(toks)
    return last


def emit_feat_block(C, key, nch, groups, n_tok, wring, evac):
    P = C.P
    wsl = C.ws.block(key, [128, nch, 128], wring)
    nb = (n_tok + 511) // 512
    last = None
    for gi, (c0, cn, rhs_fn, rhs_deps, banks) in enumerate(groups):
        deps = [wsl.fill] + list(rhs_deps)
        for b in banks[:nb]:
            deps += C.ps_users[b]
        first = True
        for bi in range(nb):
            n = min(512, n_tok - bi * 512)
            for kc in range(cn):
                last = P.op("pe", lambda e, bi=bi, kc=kc, n=n, c0=c0, cn=cn, banks=banks, rhs_fn=rhs_fn: e.matmul(
                    C.ps[:, banks[bi], 0:n], lhsT=wsl.ap[:, c0 + kc, :], rhs=rhs_fn(kc, bi * 512, n),
                    start=(kc == 0), stop=(kc == cn - 1)), deps=deps if first else (),
                    sig=(gi == len(groups) - 1 and bi == nb - 1 and kc == cn - 1))
                first = False
    wsl.users.append(last)
    toks = evac(last)
    for (c0, cn, rhs_fn, rhs_deps, banks) in groups:
        for b in banks[:nb]:
            C.ps_users[b] = list(toks)
    return last


def ps2(C, b0, n_tok):
    if n_tok <= 512:
        return C.ps[:, b0, 0:n_tok]
    return C.ps[:, b0:b0 + n_tok // 512, :]


def v2(ap, n_tok):
    if n_tok <= 512:
        return ap[:, 0:n_tok]
    return ap[:, 0:n_tok].rearrange("p (a b) -> p a b", b=512)


def build_dense(cfg, layer, stop=99):
    import contextlib
    nc = bass.Bass("TRN2", target_bir_lowering=False)
    D, T, KD, NY, FB, F = cfg.D, cfg.T, cfg.KD, cfg.NY, cfg.FB, cfg.F
    NT = T // 128
    NCB = (D + 511) // 512
    CW = min(512, D)
    x_in = nc.dram_tensor("x_own", [T, D], F32, kind="ExternalInput").ap()
    yT_in = nc.dram_tensor("yT", [NY, 128, T], BF16, kind="ExternalInput").ap()
    gains = nc.dram_tensor("gains", [4, D], F32, kind="ExternalInput").ap()
    x_out = nc.dram_tensor("x_out", [T, D], F32, kind="ExternalOutput").ap()
    mT = nc.dram_tensor("mT", [KD, 128, T], BF16).ap()
    aT = nc.dram_tensor("aT", [FB, 128, T], BF16).ap()
    o_d = nc.dram_tensor("o_d", [T, D], F32).ap()
    x_mid = nc.dram_tensor("x_mid", [T, D], F32).ap()
    with contextlib.ExitStack() as st:
        C = Ctx(nc, cfg, st)
        P = C.P
        hT = C.sb("hT", [128, KD, T], BF16)
        RA = C.sb("RA", [128, max(3 * D * 4 + D * 2 + 0, NY * T * 2, 2 * 8 * T * 2 + 3 * 8 * 512 * 2) // 2], BF16)
        NW = KD + max(cfg.HW_, cfg.PW, cfg.AW) // 128
        RW = C.sb("RW", [128, max(4 * NW * 128, 2 * D * 2)], BF16)
        sgb = C.sb("sgb", [128, 2, T], F32)
        tmb = C.sb("tmb", [128, 2, T], F32)
        mb32 = C.sb("mb32", [128, T], F32)
        mbb = C.sb("mbb", [128, 2, T], BF16)
        obb = C.sb("obb", [128, 4, CW], F32)
        ssb = C.sb("ssb", [128, 32], F32)
        ssq = C.sb("ssq", [128, NT, NCB], F32)
        rsb = C.sb("rsb", [128, NT, 4], F32)
        junk = C.sb("junk", [128, CW], BF16)
        RAf = RA[:, 0:3 * D * 2].bitcast(F32)
        xring = Ring(P, [RAf[:, 0:D], RAf[:, D:2 * D]])
        oring = Ring(P, [RAf[:, 2 * D:3 * D]])
        hbring = Ring(P, [RA[:, 3 * D * 2:3 * D * 2 + D]])
        yT = RA[:, 0:NY * T].rearrange("p (a b) -> p a b", a=NY)
        srcring = Ring(P, [RA[:, i * 8 * T:(i + 1) * 8 * T].rearrange("p (a b) -> p a b", a=8) for i in range(2)])
        o0 = 2 * 8 * T
        wmring = Ring(P, [RA[:, o0 + i * 8 * 512:o0 + (i + 1) * 8 * 512].rearrange("p (a b) -> p a b", a=8)
                          for i in range(3)])
        wring = Ring(P, [RW[:, i * NW * 128:(i + 1) * NW * 128].rearrange("p (a b) -> p a b", a=NW) for i in range(4)])
        RWf = RW[:, 0:2 * D * 2].bitcast(F32)
        gring = Ring(P, [RWf[:, 0:D], RWf[:, D:2 * D]])
        sgring = Ring(P, [sgb[:, i, :] for i in range(2)])
        tmring = Ring(P, [tmb[:, i, :] for i in range(2)])
        mbring = Ring(P, [mbb[:, i, :] for i in range(2)])
        obring = Ring(P, [obb[:, i, :] for i in range(4)])
        kgs_D = [8] * (KD // 8) + ([KD % 8] if KD % 8 else [])
        kgs_F = [8] * (FB // 8) + ([FB % 8] if FB % 8 else [])

        def load_gain(idx):
            g = gring.next()
            g.fill = bcast_load(C, "sp", g.ap, gains[idx], g.dsem, deps=g.users)
            g.users = []
            return g

        g_pre = load_gain(0)
        emit_norm_transpose(C, x_in, NT, g_pre, hT, xring, hbring, ssb)
        P.barrier()

        if stop < 1:
            C.ws.finalize()
            P.emit([])
            return nc, C.ws
        ysem = P.new_dsem()
        ytok = P.dma("sp", lambda e: e.dma_start(out=yT, in_=yT_in.rearrange("a p t -> p a t")), ysem)
        ybase = [0, cfg.HW_ // 128, (cfg.HW_ + cfg.PW) // 128]
        ycnt = [cfg.HW_ // 128, cfg.PW // 128, cfg.AW // 128]
        nb = (T + 511) // 512
        it = 0
        for j in range(KD):
            for k in range(3):
                gb = [0, 1] if it % 2 == 0 else [4, 5]
                ub = [2, 3] if it % 2 == 0 else [6, 7]
                it += 1
                sg = sgring.next()
                tm = tmring.next()

                def evac(last, k=k, gb=gb, ub=ub, sg=sg, tm=tm, j=j):
                    t_sg = P.op("act", lambda e: e.activation(out=v2(sg.ap, T), in_=ps2(C, gb[0], T), func=AF.Sigmoid),
                                deps=[last] + sg.users)
                    sg.users = []
                    if k == 0:
                        t_m = P.op("dve", lambda e: e.tensor_tensor(out=v2(mb32[:], T), in0=v2(sg.ap, T),
                                                                     in1=ps2(C, ub[0], T), op=ALU.mult),
                                   deps=[t_sg, last, getattr(C, "m_tok", None)])
                        sg.users.append(t_m)
                        C.m_tok = t_m
                        return [t_sg, t_m]
                    t_t = P.op("dve", lambda e: e.tensor_tensor(out=v2(tm.ap, T), in0=v2(sg.ap, T),
                                                                 in1=ps2(C, ub[0], T), op=ALU.mult),
                               deps=[t_sg, last] + tm.users)
                    sg.users.append(t_t)
                    if k == 1:
                        t_m = P.op("dve", lambda e: e.tensor_tensor(out=mb32[:], in0=mb32[:], in1=tm.ap, op=ALU.add),
                                   deps=[t_t, C.m_tok])
                        tm.users = [t_m]
                        C.m_tok = t_m
                        return [t_sg, t_t]
                    mbs = mbring.next()
                    t_m = P.op("dve", lambda e: e.tensor_tensor(out=mbs.ap, in0=mb32[:], in1=tm.ap, op=ALU.add),
                               deps=[t_t, C.m_tok] + mbs.users)
                    tm.users = [t_m]
                    C.m_tok = t_m
                    t_st = P.dma("sp", lambda e: e.dma_start(out=mT[j], in_=mbs.ap), mbs.dsem, deps=[t_m])
                    mbs.users = [t_st]
                    C.mT_toks.append(t_st)
                    return [t_sg, t_t]

                C.mT_toks = getattr(C, "mT_toks", [])
                groups = [
                    (0, KD, lambda kc, t0, n: hT[:, kc, t0:t0 + n], [], gb),
                    (KD, ycnt[k], lambda kc, t0, n, k=k: yT[:, ybase[k] + kc, t0:t0 + n], [ytok], ub),
                ]
                emit_feat_block(C, ("gu", layer, k, j), NW, groups, T, wring, evac)
        P.barrier(extra=C.mT_toks)

        if stop < 2:
            C.ws.finalize()
            P.emit([])
            return nc, C.ws
        def proj_out(srcT, kgs, wkey):
            for cb in range(NCB):
                def lhs_deps(g, cb=cb):
                    sl = srcring.next()
                    k0 = sum(kgs[:g])
                    gs = kgs[g]
                    sl.fill = P.dma("sp", lambda e, sl=sl, k0=k0, gs=gs: e.dma_start(
                        out=sl.ap[:, 0:gs, :], in_=srcT[k0:k0 + gs].rearrange("a p t -> p a t")), sl.dsem,
                        deps=sl.users)
                    sl.users = []
                    return [sl.fill], (lambda tok, sl=sl: sl.users.append(tok)), sl

                def evac(last, cb=cb):
                    toks = []
                    for tt in range(NT):
                        ob = obring.next()
                        t_c = P.op("dve", lambda e, ob=ob, tt=tt: e.tensor_copy(out=ob.ap[:, 0:CW], in_=C.ps[:, tt, 0:CW]),
                                   deps=[last] + ob.users)
                        t_s = P.op("act", lambda e, tt=tt, cb=cb: e.activation(
                            out=junk[:], in_=C.ps[:, tt, 0:CW], func=AF.Square, accum_out=ssq[:, tt, cb:cb + 1]),
                            deps=[last, t_c])
                        t_st = P.dma("sp", lambda e, ob=ob, tt=tt, cb=cb: e.dma_start(
                            out=o_d[tt * 128:(tt + 1) * 128, cb * CW:(cb + 1) * CW], in_=ob.ap[:, 0:CW]), ob.dsem,
                            deps=[t_c])
                        ob.users = [t_st]
                        C.o_toks.append(t_st)
                        toks += [t_c, t_s]
                    return toks

                emit_proj_tok(C, lambda g, cb=cb: (wkey, layer, cb, g), kgs,
                              lambda sl, j, tt: sl.ap[:, j, tt * 128:(tt + 1) * 128],
                              lhs_deps, NT, CW, wmring, evac)

        def residual_pre(x_src, x_dst, g_post):
            def pre(tt):
                xs = xring.next()
                xs.fill = P.dma("sp", lambda e, xs=xs, tt=tt: e.dma_start(
                    out=xs.ap, in_=x_src[tt * 128:(tt + 1) * 128, :]), xs.dsem, deps=xs.users)
                xs.users = []
                os_ = oring.next()
                os_.fill = P.dma("sp", lambda e, os_=os_, tt=tt: e.dma_start(
                    out=os_.ap, in_=o_d[tt * 128:(tt + 1) * 128, :]), os_.dsem, deps=os_.users)
                os_.users = []
                r = rsb[:, tt, :]
                t1 = P.op("dve", lambda e, tt=tt, r=r: e.reduce_sum(out=r[:, 0:1], in_=ssq[:, tt, :], axis=AX.X))
                t3 = emit_rstd(C, r[:, 0:1], r[:, 1:2], r[:, 2:3], 1.0 / D, [t1])
                t4 = P.op("dve", lambda e, os_=os_, r=r: e.scalar_tensor_tensor(
                    out=os_.ap, in0=os_.ap, scalar=r[:, 2:3], in1=g_post.ap, op0=ALU.mult, op1=ALU.mult),
                    deps=[t3, os_.fill, g_post.fill])
                t5 = P.op("dve", lambda e, os_=os_, xs=xs: e.tensor_tensor(out=xs.ap, in0=xs.ap, in1=os_.ap, op=ALU.add),
                          deps=[t4, xs.fill])
                os_.users = [t5]
                t6 = P.dma("sp", lambda e, xs=xs, tt=tt: e.dma_start(out=x_dst[tt * 128:(tt + 1) * 128, :], in_=xs.ap),
                           xs.dsem, deps=[t5])
                xs.fill = t5
                xs.users = [t6]
                C.x_toks.append(t6)
                return xs
            return pre

        C.o_toks = []
        C.x_toks = []
        proj_out(mT, kgs_D, "wo")
        P.barrier(extra=C.o_toks)
        if stop < 3:
            C.ws.finalize()
            P.emit([])
            return nc, C.ws
        g_post = load_gain(1)
        g_fpre = load_gain(2)
        emit_norm_transpose(C, None, NT, g_fpre, hT, xring, hbring, ssb, pre=residual_pre(x_in, x_mid, g_post))
        P.barrier(extra=C.x_toks)
        if stop < 4:
            C.ws.finalize()
            P.emit([])
            return nc, C.ws
        C.aT_toks = []
        it = 0
        for hb_ in range(FB):
            gb = [0, 1] if it % 2 == 0 else [4, 5]
            ub = [2, 3] if it % 2 == 0 else [6, 7]
            it += 1
            sg = sgring.next()
            st_ = {}

            def evac_g(last, sg=sg, gb=gb, st_=st_):
                t_sg = P.op("act", lambda e: e.activation(out=v2(sg.ap, T), in_=ps2(C, gb[0], T), func=AF.Silu),
                            deps=[last] + sg.users)
                sg.users = []
                st_["sg"] = t_sg
                return [t_sg]

            def evac_u(last, sg=sg, ub=ub, st_=st_, hb_=hb_):
                mbs = mbring.next()
                t_a = P.op("dve", lambda e: e.tensor_tensor(out=v2(mbs.ap, T), in0=v2(sg.ap, T), in1=ps2(C, ub[0], T),
                                                             op=ALU.mult), deps=[last, st_["sg"]] + mbs.users)
                sg.users.append(t_a)
                t_st = P.dma("sp", lambda e: e.dma_start(out=aT[hb_], in_=mbs.ap), mbs.dsem, deps=[t_a])
                mbs.users = [t_st]
                C.aT_toks.append(t_st)
                return [t_a]

            emit_feat_block(C, ("fg", layer, hb_), KD, [(0, KD, lambda kc, t0, n: hT[:, kc, t0:t0 + n], [], gb)],
                            T, wring, evac_g)
            emit_feat_block(C, ("fu", layer, hb_), KD, [(0, KD, lambda kc, t0, n: hT[:, kc, t0:t0 + n], [], ub)],
                            T, wring, evac_u)
        P.barrier(extra=C.aT_toks)
        if stop < 5:
            C.ws.finalize()
            P.emit([])
            return nc, C.ws
        C.o_toks = []
        C.x_toks = []
        proj_out(aT, kgs_F, "fd")
        P.barrier(extra=C.o_toks)
        g_fpost = load_gain(3)
        pre = residual_pre(x_mid, x_out, g_fpost)
        for tt in range(NT):
            pre(tt)
        C.ws.finalize()
        P.emit(C.x_toks)
    return nc, C.ws


def dense_getter(cfg, W):
    D, KD = cfg.D, cfg.KD
    NW = KD + max(cfg.HW_, cfg.PW, cfg.AW) // 128
    CW = min(512, D)
    ups = ["w_up_a", "w_up_b", "w_up_c"]

    def chunks(mat, c0, c1):
        sub = mat[:, c0:c1]
        return sub.reshape(-1, 128, c1 - c0).transpose(1, 0, 2)

    def get(key):
        kind, l = key[0], key[1]
        if kind == "gu":
            k, j = key[2], key[3]
            g0 = cfg.in_off[8] + k * D
            out = np.zeros((128, NW, 128), np.float32)
            out[:, :KD] = chunks(W["w_in"][l], g0 + j * 128, g0 + (j + 1) * 128)
            up = chunks(W[ups[k]][l], j * 128, (j + 1) * 128)
            out[:, KD:KD + up.shape[1]] = up
            return out
        if kind in ("wo", "fd"):
            cb, g = key[2], key[3]
            mat = W["w_out"][l] if kind == "wo" else W["w_ffn_down"][l]
            K = mat.shape[0] // 128
            kgs = [8] * (K // 8) + ([K % 8] if K % 8 else [])
            k0 = sum(kgs[:g])
            return chunks(mat[k0 * 128:(k0 + kgs[g]) * 128], cb * CW, (cb + 1) * CW)
        if kind == "fg":
            return chunks(W["w_ffn_gate"][l], key[2] * 128, (key[2] + 1) * 128)
        if kind == "fu":
            return chunks(W["w_ffn_up"][l], key[2] * 128, (key[2] + 1) * 128)
        raise KeyError(key)
    return get


def stream_inputs(cfg, ws, getter, n_cores=8):
    tot = ws.total_padded()
    buf = pack_stream(ws.plan, tot, getter)
    if not cfg.gather:
        return [buf] * n_cores
    nch = tot // CH_ELEMS
    v = buf.reshape(nch, n_cores, CH_ELEMS // n_cores)
    return [np.ascontiguousarray(v[:, r, :]) for r in range(n_cores)]


class Buf:
    def __init__(s, P, ap, dma=False):
        s.P, s.ap = P, ap
        s.w = None
        s.r = []
        s.dsem = P.new_dsem() if dma else None

    def __getitem__(s, k):
        return s.ap[k]


def do(P, eng, fn, R=(), W=(), extra=()):
    deps = [b.w for b in R] + [b.w for b in W] + [t for b in W for t in b.r] + list(extra)
    tok = P.op(eng, fn, deps)
    for b in R:
        b.r.append(tok)
    for b in W:
        b.w = tok
        b.r = []
    return tok


def dodma(P, eng, fn, sembuf, R=(), W=(), extra=()):
    deps = [b.w for b in R] + [b.w for b in W] + [t for b in W for t in b.r] + list(extra)
    tok = P.dma(eng, fn, sembuf.dsem, deps)
    for b in R:
        b.r.append(tok)
    for b in W:
        b.w = tok
        b.r = []
    return tok


class Pool_:
    def __init__(s, bufs):
        s.bufs, s.i = bufs, 0

    def get(s):
        b = s.bufs[s.i % len(s.bufs)]
        s.i += 1
        return b


def build_mixer(cfg, layer):
    import contextlib
    nc = bass.Bass("TRN2", target_bir_lowering=False)
    D, S, T, KD = cfg.D, cfg.S, cfg.T, cfg.KD
    hh, gh, ah, NYH = cfg.hh, cfg.gh, cfg.ah, cfg.NYH
    NTS = S // 128
    NT = T // 128
    HWc, PWc, AWc = hh * 128, gh * 256, ah * 128
    nfb = 2 * hh + 2 * ah
    lam_init = 0.8 - 0.6 * math.exp(-0.3 * layer)
    x_full = nc.dram_tensor("x_full", [S, D], F32, kind="ExternalInput").ap()
    gain = nc.dram_tensor("gain", [D], F32, kind="ExternalInput").ap()
    lbl = nc.dram_tensor("lb_logits", [cfg.depth, HWc], F32, kind="ExternalInput").ap()
    hgain = nc.dram_tensor("hg_gain", [HWc], F32, kind="ExternalInput").ap()
    pscale = nc.dram_tensor("pool_scale", [PWc], F32, kind="ExternalInput").ap()
    poolP = nc.dram_tensor("poolP", [gh, 3, 128, 128], F32, kind="ExternalInput").ap()
    lamp = nc.dram_tensor("lam_p", [256], F32, kind="ExternalInput").ap()
    subln = nc.dram_tensor("subln", [128], F32, kind="ExternalInput").ap()
    alq = nc.dram_tensor("alibi_q", [ah, 4, S], BF16, kind="ExternalInput").ap()
    alk = nc.dram_tensor("alibi_k", [ah, 4, S], BF16, kind="ExternalInput").ap()
    consts = nc.dram_tensor("consts", [5, 128, 128], F32, kind="ExternalInput").ap()
    yT_out = nc.dram_tensor("yT_out", [NYH, 128, S], BF16, kind="ExternalOutput").ap()
    pT = nc.dram_tensor("pT", [nfb, 128, S], F32).ap()
    pk_w = [HWc, HWc, HWc, PWc, AWc]
    pk = [nc.dram_tensor("pk%d" % i, [S, w], F32).ap() for i, w in enumerate(pk_w)]
    with contextlib.ExitStack() as st:
        C = Ctx(nc, cfg, st)
        P = C.P
        sizes = [KD * T, (2 * D * 4 + D * 2) // 2, 2 * D, 3 * KD * 128, 3 * 8 * 512, 4 * T, 8 * 512]
        NBIG = max(sum(sizes), 68 * 1024)
        BIG = C.sb("BIG", [128, NBIG], BF16)
        offs = np.concatenate([[0], np.cumsum(sizes)]).tolist()
        hT = BIG[:, offs[0]:offs[1]].rearrange("p (a b) -> p a b", a=KD)
        RA = BIG[:, offs[1]:offs[2]]
        RAf = RA[:, 0:2 * D * 2].bitcast(F32)
        xring = Ring(P, [RAf[:, 0:D], RAf[:, D:2 * D]])
        hbring = Ring(P, [RA[:, 2 * D * 2:2 * D * 2 + D]])
        gsl = Slot(BIG[:, offs[2]:offs[3]].bitcast(F32), P.new_dsem())
        ssb = C.sb("ssb", [128, 32], F32)
        wr_t = BIG[:, offs[3]:offs[4]].rearrange("p (i a b) -> p i a b", i=3, a=KD)
        wring = Ring(P, [wr_t[:, i] for i in range(3)])
        wm_t = BIG[:, offs[4]:offs[5]].rearrange("p (i a b) -> p i a b", i=3, a=8)
        wmring = Ring(P, [wm_t[:, i] for i in range(3)])
        stg_t = BIG[:, offs[5]:offs[6]].bitcast(F32).rearrange("p (i t) -> p i t", i=2)
        stgring = Ring(P, [stg_t[:, i, :] for i in range(2)])
        ob_t = BIG[:, offs[6]:offs[7]].bitcast(F32).rearrange("p (i t) -> p i t", i=4)
        obring = Ring(P, [ob_t[:, i, :] for i in range(4)])
        kgs_D = [8] * (KD // 8) + ([KD % 8] if KD % 8 else [])
        gsl.fill = bcast_load(C, "sp", gsl.ap, gain, gsl.dsem)
        st_toks = []
        for th in range(2):
            emit_norm_transpose(C, x_full[th * T:(th + 1) * T, :], NT, gsl, hT, xring, hbring, ssb)
            P.barrier()
            it = 0
            for blk in range(nfb):
                banks = [0, 1] if it % 2 == 0 else [2, 3]
                it += 1
                sg = stgring.next()

                def evac(last, sg=sg, banks=banks, blk=blk, th=th, it=it):
                    eng = "act" if it % 2 == 0 else "dve"
                    if eng == "act":
                        t_c = P.op("act", lambda e: e.copy(out=v2(sg.ap, T), in_=ps2(C, banks[0], T)), deps=[last] + sg.users)
                    else:
                        t_c = P.op("dve", lambda e: e.tensor_copy(out=v2(sg.ap, T), in_=ps2(C, banks[0], T)),
                                   deps=[last] + sg.users)
                    t_st = P.dma("sp", lambda e: e.dma_start(out=pT[blk][:, th * T:(th + 1) * T], in_=sg.ap), sg.dsem,
                                 deps=[t_c])
                    sg.users = [t_st]
                    st_toks.append(t_st)
                    return [t_c]
                emit_feat_block(C, ("mf", layer, blk), KD, [(0, KD, lambda kc, t0, n: hT[:, kc, t0:t0 + n], [], banks)],
                                T, wring, evac)
            for pi, w in enumerate(pk_w):
                def evac2(last, pi=pi, w=w, th=th):
                    toks = []
                    for tt in range(NT):
                        ob = obring.next()
                        eng = "act" if tt % 2 == 0 else "dve"
                        if eng == "act":
                            t_c = P.op("act", lambda e, ob=ob, tt=tt: e.copy(out=ob.ap[:, 0:w], in_=C.ps[:, tt, 0:w]),
                                       deps=[last] + ob.users)
                        else:
                            t_c = P.op("dve", lambda e, ob=ob, tt=tt: e.tensor_copy(out=ob.ap[:, 0:w], in_=C.ps[:, tt, 0:w]),
                                       deps=[last] + ob.users)
                        t_st = P.dma("sp", lambda e, ob=ob, tt=tt: e.dma_start(
                            out=pk[pi][th * T + tt * 128:th * T + (tt + 1) * 128, :], in_=ob.ap[:, 0:w]), ob.dsem, deps=[t_c])
                        ob.users = [t_st]
                        st_toks.append(t_st)
                        toks.append(t_c)
                    return toks
                emit_proj_tok(C, lambda g, pi=pi: ("mt", layer, pi, g), kgs_D,
                              lambda k0, j, tt: hT[:, k0 + j, tt * 128:(tt + 1) * 128],
                              lambda g: ([], (lambda tok: None), sum(kgs_D[:g])), NT, w, wmring, evac2)
            P.barrier(extra=st_toks)

        bank = [Buf(P, C.ps[:, b, :]) for b in range(8)]
        bankb = [Buf(P, C.psb[:, b, :]) for b in range(8)]
        for b in range(8):
            bankb[b] = bank[b]

        bump = [0]

        def mk(name, shape, dt, n=1, dma=False):
            per = int(np.prod(shape)) * (2 if dt == F32 else 1)
            per = (per + 15) // 16 * 16
            bufs = []
            for i in range(n):
                v = BIG[:, bump[0]:bump[0] + per]
                bump[0] += per
                assert bump[0] <= NBIG, ("mixer sbuf overflow", name, bump[0], NBIG)
                if dt == F32:
                    v = v.bitcast(F32)
                v = v[:, 0:int(np.prod(shape))]
                if len(shape) == 2:
                    v = v.rearrange("p (a b) -> p a b", a=shape[0])
                bufs.append(Buf(P, v, dma=dma))
            return bufs

        ysem = Buf(P, None, dma=True)
        cst = mk("cst", [128], F32, 5, dma=True)
        for i in range(5):
            dodma(P, "sp", lambda e, i=i: e.dma_start(out=cst[i].ap, in_=consts[i]), cst[i], W=[cst[i]])
        cM1, cL, cM4, mBD, mTri32 = cst
        mTri = mk("mTri", [128], BF16)[0]
        do(P, "dve", lambda e: e.tensor_copy(out=mTri.ap, in_=mTri32.ap), R=[mTri32], W=[mTri])
        junk = mk("junkm", [512], BF16)[0]
        lbt = mk("lbt", [HWc], F32)[0]
        omlbt = mk("omlbt", [HWc], F32)[0]
        lbT = mk("lbT", [hh], F32)[0]
        omlbT = mk("omlbT", [hh], F32)[0]
        lgt = mk("lgt", [cfg.depth, HWc], F32, dma=True)[0]
        lgT = mk("lgT", [cfg.depth, hh], F32, dma=True)[0]
        tot1 = mk("tot1", [HWc], F32)[0]
        tot2 = mk("tot2", [hh], F32)[0]
        for (lg, lb_, om_, tot, src) in ((lgt, lbt, omlbt, tot1, None), (lgT, lbT, omlbT, tot2, 1)):
            if src is None:
                dodma(P, "sp", lambda e: e.dma_start(
                    out=lgt.ap.rearrange("p a b -> p (a b)"),
                    in_=lbl.rearrange("a b -> (a b)").partition_broadcast(128)), lgt, W=[lgt])
            else:
                for l_ in range(cfg.depth):
                    dodma(P, "sp", lambda e, l_=l_: e.dma_start(
                        out=lgT.ap[:, l_, :], in_=lbl[l_].rearrange("(h p) -> p h", p=128),
                        allow_slow_non_contiguous=True), lgT, W=[lgT])
            if layer == 0:
                do(P, "dve", lambda e, lb_=lb_: e.memset(lb_.ap, 0.0), W=[lb_])
            else:
                do(P, "act", lambda e, lg=lg: e.activation(out=lg.ap, in_=lg.ap, func=AF.Exp), W=[lg])
                do(P, "dve", lambda e, lg=lg, tot=tot: e.tensor_copy(out=tot.ap, in_=lg.ap[:, 0, :]), R=[lg], W=[tot])
                for l_ in range(1, cfg.depth):
                    do(P, "dve", lambda e, lg=lg, tot=tot, l_=l_: e.tensor_tensor(out=tot.ap, in0=tot.ap, in1=lg.ap[:, l_, :],
                                                                                  op=ALU.add), R=[lg], W=[tot])
                do(P, "dve", lambda e, tot=tot: e.reciprocal(out=tot.ap, in_=tot.ap), W=[tot])
                do(P, "dve", lambda e, lg=lg, lb_=lb_: e.tensor_copy(out=lb_.ap, in_=lg.ap[:, 1, :]), R=[lg], W=[lb_])
                for l_ in range(2, layer + 1):
                    do(P, "dve", lambda e, lg=lg, lb_=lb_, l_=l_: e.tensor_tensor(out=lb_.ap, in0=lb_.ap, in1=lg.ap[:, l_, :],
                                                                                  op=ALU.add), R=[lg], W=[lb_])
                do(P, "dve", lambda e, lb_=lb_, tot=tot: e.tensor_tensor(out=lb_.ap, in0=lb_.ap, in1=tot.ap, op=ALU.mult),
                   R=[tot], W=[lb_])
            do(P, "dve", lambda e, lb_=lb_, om_=om_: e.tensor_scalar(out=om_.ap, in0=lb_.ap, scalar1=-1.0, scalar2=1.0,
                                                                     op0=ALU.mult, op1=ALU.add), R=[lb_], W=[om_])
        hgt = mk("hgt", [HWc], F32, dma=True)[0]
        dodma(P, "sp", lambda e: e.dma_start(out=hgt.ap, in_=hgain.partition_broadcast(128)), hgt, W=[hgt])

        NB = 2
        ld_f = mk("ld_f", [HWc], F32, NB, dma=True)
        ld_v = mk("ld_v", [HWc], F32, NB, dma=True)
        ld_g = mk("ld_g", [HWc], F32, NB, dma=True)
        ld_fT = mk("ld_fT", [hh, 128], F32, NB, dma=True)
        ld_qT = mk("ld_qT", [hh, 128], F32, NB, dma=True)
        NB1 = 1
        w_sig = mk("w_sig", [HWc], F32, NB1)
        w_logf = mk("w_logf", [HWc], F32, NB1)
        w_kk = mk("w_kk", [HWc], F32, NB1)
        w_sigT = mk("w_sigT", [hh, 128], F32, NB1)
        w_kkT = mk("w_kkT", [hh, 128], F32, NB1)
        w_X1 = mk("w_X1", [hh, 128], F32, NB1)
        w_X1n = mk("w_X1n", [hh, 128], F32, NB1)
        w_X3 = mk("w_X3", [hh, 128], F32, NB1)
        w_X4 = mk("w_X4", [HWc], F32, NB1)
        w_qf = mk("w_qf", [hh, 128], BF16, NB)
        w_kf = mk("w_kf", [hh, 128], BF16, NB)
        w_qlo = mk("w_qlo", [hh, 128], BF16, NB)
        w_qhi = mk("w_qhi", [hh, 128], BF16, NB)
        w_kd = mk("w_kd", [HWc], BF16, NB)
        w_vb = mk("w_vb", [HWc], BF16, NB)
        w_gsg = mk("w_gsg", [HWc], F32, NB1)
        w_scm = mk("w_scm", [128], BF16, 2)
        w_yb = mk("w_yb", [128], BF16, 2)
        w_st = mk("w_st", [8], F32, 4)
        S32 = mk("S32", [128], F32, hh)
        Sbf = mk("Sbf", [128], BF16, hh)
        Sbf1 = mk("Sbf1", [128], BF16, hh)
        ysa = mk("ysa", [S], BF16, max(hh, 2 * gh, ah), dma=True)
        C.ys = ysa
        for b in w_qlo + w_qhi:
            do(P, "dve", lambda e, b=b: e.memset(b.ap, 0.0), W=[b])
        for h in range(hh):
            do(P, "dve", lambda e, h=h: e.memset(S32[h].ap, 0.0), W=[S32[h]])
            do(P, "dve", lambda e, h=h: e.memset(Sbf[h].ap, 0.0), W=[Sbf[h]])
        qscale = 128 ** -0.5
        def hg_tile(i):
            r0, r1 = i * 128, (i + 1) * 128
            bi = i % NB
            f_, v_, g_, fT_, qT_ = ld_f[bi], ld_v[bi], ld_g[bi], ld_fT[bi], ld_qT[bi]
            dodma(P, "sp", lambda e, f_=f_: e.dma_start(out=f_.ap, in_=pk[0][r0:r1, :]), f_, W=[f_])
            dodma(P, "sp", lambda e, v_=v_: e.dma_start(out=v_.ap, in_=pk[1][r0:r1, :]), v_, W=[v_])
            dodma(P, "sp", lambda e, g_=g_: e.dma_start(out=g_.ap, in_=pk[2][r0:r1, :]), g_, W=[g_])
            dodma(P, "sp", lambda e, qT_=qT_: e.dma_start(out=qT_.ap, in_=pT[0:hh, :, r0:r1].rearrange("h p t -> p h t")),
                  qT_, W=[qT_])
            dodma(P, "sp", lambda e, fT_=fT_: e.dma_start(out=fT_.ap, in_=pT[hh:2 * hh, :, r0:r1].rearrange("h p t -> p h t")),
                  fT_, W=[fT_])
            sig, logf, kk, sigT, kkT = w_sig[0], w_logf[0], w_kk[0], w_sigT[0], w_kkT[0]
            X1, X1n, X3, X4 = w_X1[0], w_X1n[0], w_X3[0], w_X4[0]
            qf, kf, qlo, qhi, kd, vb, gsg = w_qf[bi], w_kf[bi], w_qlo[bi], w_qhi[bi], w_kd[bi], w_vb[bi], w_gsg[0]
            do(P, "act", lambda e: e.activation(out=sig.ap, in_=f_.ap, func=AF.Sigmoid), R=[f_], W=[sig])
            do(P, "dve", lambda e: e.tensor_tensor(out=sig.ap, in0=sig.ap, in1=omlbt.ap, op=ALU.mult), R=[omlbt], W=[sig])
            do(P, "dve", lambda e: e.tensor_tensor(out=sig.ap, in0=sig.ap, in1=lbt.ap, op=ALU.add), R=[lbt], W=[sig])
            do(P, "act", lambda e: e.activation(out=logf.ap, in_=sig.ap, func=AF.Ln), R=[sig], W=[logf])
            do(P, "dve", lambda e: e.tensor_scalar(out=kk.ap, in0=sig.ap, scalar1=-1.0, scalar2=1.0, op0=ALU.mult, op1=ALU.add),
               R=[sig], W=[kk])
            do(P, "act", lambda e: e.activation(out=sigT.ap, in_=fT_.ap, func=AF.Sigmoid), R=[fT_], W=[sigT])
            for h in range(hh):
                do(P, "dve", lambda e, h=h: e.tensor_scalar(out=sigT.ap[:, h, :], in0=sigT.ap[:, h, :],
                                                            scalar1=omlbT.ap[:, h:h + 1], scalar2=lbT.ap[:, h:h + 1],
                                                            op0=ALU.mult, op1=ALU.add), R=[omlbT, lbT], W=[sigT])
            do(P, "dve", lambda e: e.tensor_scalar(out=kkT.ap, in0=sigT.ap, scalar1=-1.0, scalar2=1.0, op0=ALU.mult,
                                                   op1=ALU.add), R=[sigT], W=[kkT])
            for h in range(hh):
                do(P, "pe", lambda e, h=h: e.matmul(C.ps[:, 0, h * 128:(h + 1) * 128], lhsT=logf.ap[:, h * 128:(h + 1) * 128],
                                                    rhs=cM1.ap, start=True, stop=True), R=[logf, cM1], W=[bank[0]])
                do(P, "pe", lambda e, h=h: e.matmul(C.ps[:, 1, h * 128:(h + 1) * 128], lhsT=logf.ap[:, h * 128:(h + 1) * 128],
                                                    rhs=cL.ap, start=True, stop=True), R=[logf, cL], W=[bank[1]])
            do(P, "pe", lambda e: e.matmul(C.ps[:, 2, 0:HWc], lhsT=cM4.ap, rhs=logf.ap, start=True, stop=True),
               R=[logf, cM4], W=[bank[2]])
            e1v = C.ps[:, 0, 0:HWc].rearrange("p (h t) -> p h t", h=hh)
            e3v = C.ps[:, 1, 0:HWc].rearrange("p (h t) -> p h t", h=hh)
            do(P, "act", lambda e: e.activation(out=X1.ap, in_=e1v, func=AF.Exp), R=[bank[0]], W=[X1])
            do(P, "act", lambda e: e.activation(out=X1n.ap, in_=e1v, func=AF.Exp, scale=-1.0), R=[bank[0]], W=[X1n])
            do(P, "act", lambda e: e.activation(out=X3.ap, in_=e3v, func=AF.Exp), R=[bank[1]], W=[X3])
            do(P, "act", lambda e: e.activation(out=X4.ap, in_=C.ps[:, 2, 0:HWc], func=AF.Exp), R=[bank[2]], W=[X4])
            do(P, "dve", lambda e: e.scalar_tensor_tensor(out=qf.ap, in0=qT_.ap, scalar=qscale, in1=X1.ap, op0=ALU.mult,
                                                          op1=ALU.mult), R=[qT_, X1], W=[qf])
            do(P, "dve", lambda e: e.tensor_tensor(out=kf.ap, in0=kkT.ap, in1=X1n.ap, op=ALU.mult), R=[kkT, X1n], W=[kf])
            do(P, "dve", lambda e: e.scalar_tensor_tensor(out=qlo.ap[:, :, 0:64], in0=qT_.ap[:, :, 0:64], scalar=qscale,
                                                          in1=X3.ap[:, :, 0:64], op0=ALU.mult, op1=ALU.mult),
               R=[qT_, X3], W=[qlo])
            do(P, "dve", lambda e: e.scalar_tensor_tensor(out=qhi.ap[:, :, 64:128], in0=qT_.ap[:, :, 64:128], scalar=qscale,
                                                          in1=X3.ap[:, :, 64:128], op0=ALU.mult, op1=ALU.mult),
               R=[qT_, X3], W=[qhi])
            do(P, "dve", lambda e: e.tensor_tensor(out=kd.ap, in0=kk.ap, in1=X4.ap, op=ALU.mult), R=[kk, X4], W=[kd])
            do(P, "act", lambda e: e.copy(out=vb.ap, in_=v_.ap), R=[v_], W=[vb])
            do(P, "act", lambda e: e.activation(out=gsg.ap, in_=g_.ap, func=AF.Silu), R=[g_], W=[gsg])
            do(P, "dve", lambda e: e.tensor_tensor(out=gsg.ap, in0=gsg.ap, in1=hgt.ap, op=ALU.mult), R=[hgt], W=[gsg])
            for h in range(hh):
                hs = slice(h * 128, (h + 1) * 128)
                scm = w_scm[(i * hh + h) % 2]
                yb = w_yb[(i * hh + h) % 2]
                stt = w_st[(i * hh + h) % 4]
                do(P, "pe", lambda e, h=h: e.matmul(C.ps[:, 3, 0:128], lhsT=kf.ap[:, h, :], rhs=qf.ap[:, h, :], start=True,
                                                    stop=True), R=[kf, qf], W=[bank[3]])
                do(P, "dve", lambda e, scm=scm: e.tensor_tensor(out=scm.ap, in0=C.ps[:, 3, 0:128], in1=mBD.ap, op=ALU.mult),
                   R=[bank[3], mBD], W=[scm])
                do(P, "pe", lambda e, scm=scm, hs=hs: e.matmul(C.ps[:, 4, 0:128], lhsT=scm.ap, rhs=vb.ap[:, hs], start=True,
                                                               stop=False), R=[scm, vb], W=[bank[4]])
                do(P, "pe", lambda e, h=h: e.matmul(C.ps[:, 4, 0:128], lhsT=qlo.ap[:, h, :], rhs=Sbf[h].ap, start=False,
                                                    stop=False), R=[qlo, Sbf[h]], W=[bank[4]])
                do(P, "pe", lambda e, hs=hs: e.matmul(C.ps[:, 5, 0:128], lhsT=kd.ap[0:64, hs], rhs=vb.ap[0:64, hs], start=True,
                                                      stop=True), R=[kd, vb], W=[bank[5]])
                do(P, "dve", lambda e, h=h: e.scalar_tensor_tensor(out=S32[h].ap, in0=S32[h].ap, scalar=X3.ap[:, h, 63:64],
                                                                   in1=C.ps[:, 5, 0:128], op0=ALU.mult, op1=ALU.add),
                   R=[X3, bank[5]], W=[S32[h]])
                do(P, "act", lambda e, h=h: e.copy(out=Sbf1[h].ap, in_=S32[h].ap), R=[S32[h]], W=[Sbf1[h]])
                do(P, "pe", lambda e, h=h: e.matmul(C.ps[:, 4, 0:128], lhsT=qhi.ap[:, h, :], rhs=Sbf1[h].ap, start=False,
                                                    stop=True), R=[qhi, Sbf1[h]], W=[bank[4]])
                do(P, "pe", lambda e, hs=hs: e.matmul(C.ps[:, 5, 0:128], lhsT=kd.ap[64:128, hs], rhs=vb.ap[64:128, hs],
                                                      start=True, stop=True), R=[kd, vb], W=[bank[5]])
                do(P, "dve", lambda e, h=h: e.scalar_tensor_tensor(out=S32[h].ap, in0=S32[h].ap, scalar=X3.ap[:, h, 127:128],
                                                                   in1=C.ps[:, 5, 0:128], op0=ALU.mult, op1=ALU.add),
                   R=[X3, bank[5]], W=[S32[h]])
                do(P, "act", lambda e, h=h: e.copy(out=Sbf[h].ap, in_=S32[h].ap), R=[S32[h]], W=[Sbf[h]])
                do(P, "act", lambda e, stt=stt: e.activation(out=junk.ap[:, 0:128], in_=C.ps[:, 4, 0:128], func=AF.Square,
                                                             accum_out=stt.ap[:, 0:1]), R=[bank[4]], W=[junk, stt])
                do(P, "act", lambda e, stt=stt: e.activation(out=stt.ap[:, 1:2], in_=stt.ap[:, 0:1], func=AF.Sqrt,
                                                             bias=C.epsc[:, 0:1], scale=1.0 / 128), W=[stt])
                do(P, "dve", lambda e, stt=stt: e.reciprocal(out=stt.ap[:, 2:3], in_=stt.ap[:, 1:2]), W=[stt])
                do(P, "dve", lambda e, stt=stt, yb=yb, hs=hs: e.scalar_tensor_tensor(
                    out=yb.ap, in0=C.ps[:, 4, 0:128], scalar=stt.ap[:, 2:3], in1=gsg.ap[:, hs], op0=ALU.mult, op1=ALU.mult),
                    R=[bank[4], stt, gsg], W=[yb])
                do(P, "pe", lambda e, yb=yb: e.transpose(out=C.psb[:, 6, 0:128], in_=yb.ap, identity=C.ident[:]),
                   R=[yb], W=[bank[6]])
                do(P, "act", lambda e, h=h: e.copy(out=ysa[h].ap[:, r0:r1], in_=C.psb[:, 6, 0:128]), R=[bank[6]], W=[ysa[h]])
        for i in range(NTS):
            hg_tile(i)
        out_toks = []
        for h in range(hh):
            out_toks.append(dodma(P, "sp", lambda e, h=h: e.dma_start(out=yT_out[h], in_=ysa[h].ap), ysa[h], R=[ysa[h]]))
        C.hg_state = (bank, mk, junk, mTri, out_toks, pT, pk, yT_out)
        build_mixer_rest(C, cfg, layer, lam_init, poolP, pscale, lamp, subln, alq, alk)
        C.ws.finalize()
        P.emit(C.out_toks)
    return nc, C.ws


def build_mixer_rest(C, cfg, layer, lam_init, poolP, pscale, lamp, subln, alq, alk):
    P = C.P
    bank, mk, junk, mTri, out_toks, pT, pk, yT_out = C.hg_state
    S, hh, gh, ah = cfg.S, cfg.hh, cfg.gh, cfg.ah
    NTS = S // 128
    wpt = C.sb("wpt", [128, gh, 2, 256], BF16)
    wpring = Ring(P, [wpt[:, g] for g in range(gh)])
    pP = mk("pP", [3, 128], F32, gh, dma=True)
    pscT = mk("pscT", [2 * gh], F32, dma=True)[0]
    dodma(P, "sp", lambda e: e.dma_start(out=pscT.ap, in_=pscale.rearrange("(c p) -> p c", p=128),
                                         allow_slow_non_contiguous=True), pscT, W=[pscT])
    u_b = mk("u_b", [256], F32, 3, dma=True)
    pl_b = mk("pl_b", [2, 128], BF16, 2)
    ysb = C.ys
    for gi in range(gh):
        wsl = C.ws.block(("pw", layer, gi), [128, 2, 256], wpring)
        wbuf = Buf(P, wsl.ap)
        wbuf.w = wsl.fill
        dodma(P, "sp", lambda e, gi=gi: e.dma_start(out=pP[gi].ap, in_=poolP[gi].rearrange("a s t -> s a t")), pP[gi], W=[pP[gi]])
        prev = None
        for i in range(NTS):
            u = u_b[(gi * NTS + i) % 3]
            dodma(P, "sp", lambda e, u=u, i=i, gi=gi: e.dma_start(
                out=u.ap, in_=pk[3][i * 128:(i + 1) * 128, gi * 256:(gi + 1) * 256]), u, W=[u])
            for cb in range(2):
                do(P, "pe", lambda e, u=u, cb=cb, i=i, gi=gi: e.matmul(
                    C.ps[:, 0, cb * 128:(cb + 1) * 128], lhsT=u.ap[:, cb * 128:(cb + 1) * 128],
                    rhs=pP[gi].ap[:, 0 if i == 0 else 1, :], start=True, stop=(i == 0)), R=[u, pP[gi]], W=[bank[0]])
                if i > 0:
                    do(P, "pe", lambda e, prev=prev, cb=cb, gi=gi: e.matmul(
                        C.ps[:, 0, cb * 128:(cb + 1) * 128], lhsT=prev.ap[:, cb * 128:(cb + 1) * 128],
                        rhs=pP[gi].ap[:, 2, :], start=False, stop=True), R=[prev, pP[gi]], W=[bank[0]])
            pl = pl_b[i % 2]
            do(P, "act", lambda e, pl=pl: e.copy(out=pl.ap, in_=C.ps[:, 0, 0:256].rearrange("p (a b) -> p a b", a=2)),
               R=[bank[0]], W=[pl])
            for db in range(2):
                for cb in range(2):
                    do(P, "pe", lambda e, pl=pl, db=db, cb=cb, wbuf=wbuf: e.matmul(
                        C.ps[:, 1, db * 128:(db + 1) * 128], lhsT=wbuf.ap[:, cb, db * 128:(db + 1) * 128], rhs=pl.ap[:, cb, :],
                        start=(cb == 0), stop=(cb == 1)), R=[pl, wbuf], W=[bank[1]])
            for db in range(2):
                yb_ = ysb[gi * 2 + db]
                do(P, "dve", lambda e, yb_=yb_, db=db, i=i, gi=gi: e.tensor_scalar(
                    out=yb_.ap[:, i * 128:(i + 1) * 128], in0=C.ps[:, 1, db * 128:(db + 1) * 128],
                    scalar1=pscT.ap[:, gi * 2 + db:gi * 2 + db + 1], scalar2=None, op0=ALU.mult),
                    R=[bank[1], pscT], W=[yb_])
            prev = u
    for c in range(2 * gh):
        out_toks.append(dodma(P, "sp", lambda e, c=c: e.dma_start(out=yT_out[hh + c], in_=ysb[c].ap), ysb[c], R=[ysb[c]]))

    lamb = mk("lamb", [256], F32, dma=True)[0]
    dodma(P, "sp", lambda e: e.dma_start(out=lamb.ap, in_=lamp.partition_broadcast(128)), lamb, W=[lamb])
    lw = mk("lw", [8], F32)[0]
    for j in range(2):
        do(P, "dve", lambda e, j=j: e.tensor_tensor(out=junk.ap[:, 0:64].bitcast(F32) if False else lamb.ap[:, j * 128:j * 128 + 64],
                                                     in0=lamb.ap[:, j * 128:j * 128 + 64],
                                                     in1=lamb.ap[:, j * 128 + 64:j * 128 + 128], op=ALU.mult), W=[lamb])
        do(P, "dve", lambda e, j=j: e.reduce_sum(out=lw.ap[:, j:j + 1], in_=lamb.ap[:, j * 128:j * 128 + 64], axis=AX.X),
           R=[lamb], W=[lw])
    do(P, "act", lambda e: e.activation(out=lw.ap[:, 2:4], in_=lw.ap[:, 0:2], func=AF.Exp), W=[lw])
    do(P, "dve", lambda e: e.tensor_tensor(out=lw.ap[:, 4:5], in0=lw.ap[:, 3:4], in1=lw.ap[:, 2:3], op=ALU.subtract), W=[lw])
    do(P, "dve", lambda e: e.tensor_scalar(out=lw.ap[:, 5:6], in0=lw.ap[:, 4:5], scalar1=-lam_init, scalar2=None, op0=ALU.add),
       W=[lw])
    gsub = mk("gsub", [128], F32, dma=True)[0]
    dodma(P, "sp", lambda e: e.dma_start(out=gsub.ap, in_=subln.partition_broadcast(128)), gsub, W=[gsub])
    do(P, "dve", lambda e: e.tensor_scalar(out=gsub.ap, in0=gsub.ap, scalar1=1.0 - lam_init, scalar2=None, op0=ALU.mult),
       W=[gsub])
    qk_raw = mk("qk_raw", [8 * S], BF16)[0].ap
    qa_t = qk_raw[0:68, 0:4 * S].rearrange("p (a m s) -> p a m s", a=2, m=2)
    ka_t = qk_raw[0:68, 4 * S:8 * S].rearrange("p (a m s) -> p a m s", a=2, m=2)
    qa = [[Buf(P, qa_t[:, a, m, :], dma=True) for m in range(2)] for a in range(2)]
    ka = [[Buf(P, ka_t[:, a, m, :], dma=True) for m in range(2)] for a in range(2)]
    va = mk("va", [NTS, 129], BF16, 2, dma=True)
    for b in va:
        do(P, "dve", lambda e, b=b: e.memset(b.ap[:, :, 128:129], 1.0), W=[b])
    pt_b = mk("pt_b", [512], BF16, 2)
    om = mk("om", [4, 128], F32, 2)
    rd = mk("rd", [8], F32, 4)
    w_st = mk("w_st2", [8], F32, 4)
    w_o = mk("w_o", [128], F32, 2)
    w_yb = mk("w_yb2", [128], BF16, 2)
    ysc = C.ys
    RNG = min(512, S)
    NR = S // RNG
    TPR = RNG // 128
    cnt = 0
    for h in range(ah):
        a = h % 2
        for m in range(2):
            dodma(P, "pool", lambda e, a=a, m=m, h=h: e.dma_start(out=qa[a][m].ap[0:64, :],
                                                              in_=pT[2 * hh + h][m * 64:(m + 1) * 64, :]), qa[a][m], W=[qa[a][m]])
            dodma(P, "pool", lambda e, a=a, m=m, h=h: e.dma_start(out=qa[a][m].ap[64:68, :], in_=alq[h]), qa[a][m], W=[qa[a][m]])
            dodma(P, "pool", lambda e, a=a, m=m, h=h: e.dma_start(out=ka[a][m].ap[0:64, :],
                                                              in_=pT[2 * hh + ah + h][m * 64:(m + 1) * 64, :]), ka[a][m], W=[ka[a][m]])
            dodma(P, "pool", lambda e, a=a, m=m, h=h: e.dma_start(out=ka[a][m].ap[64:68, :], in_=alk[h]), ka[a][m], W=[ka[a][m]])
        dodma(P, "pool", lambda e, a=a, h=h: e.dma_start(
            out=va[a].ap[:, :, 0:128], in_=pk[4][:, h * 128:(h + 1) * 128].rearrange("(i p) v -> p i v", p=128)), va[a], W=[va[a]])
        for tr in range(NR):
            for m in range(2):
                omb = om[m]
                for sb in range(0, TPR * tr + TPR):
                    t_lo = max(sb * 128, tr * RNG)
                    N = (tr + 1) * RNG - t_lo
                    sbank = cnt % 2
                    ptb = pt_b[cnt % 2]
                    cnt += 1
                    do(P, "pe", lambda e, a=a, m=m, sb=sb, t_lo=t_lo, N=N, sbank=sbank: e.matmul(
                        C.ps[:, sbank, 0:N], lhsT=ka[a][m].ap[:, sb * 128:(sb + 1) * 128], rhs=qa[a][m].ap[:, t_lo:t_lo + N],
                        start=True, stop=True), R=[ka[a][m], qa[a][m]], W=[bank[sbank]])
                    do(P, "act", lambda e, ptb=ptb, N=N, sbank=sbank: e.activation(
                        out=ptb.ap[:, 0:N], in_=C.ps[:, sbank, 0:N], func=AF.Exp, scale=0.125), R=[bank[sbank]], W=[ptb])
                    if sb * 128 >= tr * RNG:
                        do(P, "dve", lambda e, ptb=ptb: e.tensor_tensor(out=ptb.ap[:, 0:128], in0=ptb.ap[:, 0:128], in1=mTri.ap,
                                                                        op=ALU.mult), R=[mTri], W=[ptb])
                    for q in range(N // 128):
                        gt = t_lo // 128 + q
                        loc = gt - TPR * tr
                        do(P, "pe", lambda e, ptb=ptb, q=q, loc=loc, sb=sb, gt=gt, a=a: e.matmul(
                            C.ps[:, 2 + loc, 0:129], lhsT=ptb.ap[:, q * 128:(q + 1) * 128], rhs=va[a].ap[:, sb, :],
                            start=(sb == 0), stop=(sb == gt)), R=[ptb, va[a]], W=[bank[2 + loc]])
                for loc in range(TPR):
                    r_ = rd[(loc + m * TPR) % 4]
                    do(P, "dve", lambda e, r_=r_, loc=loc: e.reciprocal(out=r_.ap[:, 0:1], in_=C.ps[:, 2 + loc, 128:129]),
                       R=[bank[2 + loc]], W=[r_])
                    do(P, "dve", lambda e, r_=r_, loc=loc, omb=omb: e.tensor_scalar(
                        out=omb.ap[:, loc, :], in0=C.ps[:, 2 + loc, 0:128], scalar1=r_.ap[:, 0:1], scalar2=None, op0=ALU.mult),
                        R=[bank[2 + loc], r_], W=[omb])
            for loc in range(TPR):
                gt = TPR * tr + loc
                o_ = w_o[loc % 2]
                stt = w_st[loc % 4]
                yb = w_yb[loc % 2]
                do(P, "dve", lambda e, o_=o_, loc=loc: e.scalar_tensor_tensor(
                    out=o_.ap, in0=om[1].ap[:, loc, :], scalar=lw.ap[:, 5:6], in1=om[0].ap[:, loc, :], op0=ALU.mult, op1=ALU.add),
                    R=[om[0], om[1], lw], W=[o_])
                do(P, "act", lambda e, o_=o_, stt=stt: e.activation(out=junk.ap[:, 0:128], in_=o_.ap, func=AF.Square,
                                                                   accum_out=stt.ap[:, 0:1]), R=[o_], W=[junk, stt])
                do(P, "act", lambda e, stt=stt: e.activation(out=stt.ap[:, 1:2], in_=stt.ap[:, 0:1], func=AF.Sqrt,
                                                             bias=C.epsc[:, 0:1], scale=1.0 / 128), W=[stt])
                do(P, "dve", lambda e, stt=stt: e.reciprocal(out=stt.ap[:, 2:3], in_=stt.ap[:, 1:2]), W=[stt])
                do(P, "dve", lambda e, o_=o_, stt=stt, yb=yb: e.scalar_tensor_tensor(
                    out=yb.ap, in0=o_.ap, scalar=stt.ap[:, 2:3], in1=gsub.ap, op0=ALU.mult, op1=ALU.mult),
                    R=[o_, stt, gsub], W=[yb])
                do(P, "pe", lambda e, yb=yb: e.transpose(out=C.psb[:, 6, 0:128], in_=yb.ap, identity=C.ident[:]),
                   R=[yb], W=[bank[6]])
                do(P, "act", lambda e, h=h, gt=gt: e.copy(out=ysc[h].ap[:, gt * 128:(gt + 1) * 128], in_=C.psb[:, 6, 0:128]),
                   R=[bank[6]], W=[ysc[h]])
    for h in range(ah):
        out_toks.append(dodma(P, "sp", lambda e, h=h: e.dma_start(out=yT_out[hh + 2 * gh + h], in_=ysc[h].ap), ysc[h], R=[ysc[h]]))
    C.out_toks = out_toks


POOL_WINDOWS = (2, 4, 8, 16)


def mixer_cols(cfg, hf):
    io = cfg.in_off
    hh, gh, ah = cfg.hh, cfg.gh, cfg.ah
    fb = []
    for base in (io[0], io[1]):
        for h in range(hh):
            fb.append(np.arange(base + (hf * hh + h) * 128, base + (hf * hh + h + 1) * 128))
    for base in (io[5], io[6]):
        for h in range(ah):
            fb.append(np.arange(base + (hf * ah + h) * 128, base + (hf * ah + h + 1) * 128))
    pieces = [np.arange(io[1] + hf * hh * 128, io[1] + (hf + 1) * hh * 128),
              np.arange(io[2] + hf * hh * 128, io[2] + (hf + 1) * hh * 128),
              np.arange(io[3] + hf * hh * 128, io[3] + (hf + 1) * hh * 128),
              np.arange(io[4] + hf * gh * 256, io[4] + (hf + 1) * gh * 256),
              np.arange(io[7] + hf * ah * 128, io[7] + (hf + 1) * ah * 128)]
    return fb, pieces


def mixer_getter(cfg, W, hf):
    fb, pieces = mixer_cols(cfg, hf)
    KD = cfg.KD
    kgs = [8] * (KD // 8) + ([KD % 8] if KD % 8 else [])

    def get(key):
        kind, l = key[0], key[1]
        if kind == "mf":
            sub = W["w_in"][l][:, fb[key[2]]]
            return sub.reshape(KD, 128, 128).transpose(1, 0, 2)
        if kind == "mt":
            pi, g = key[2], key[3]
            k0 = sum(kgs[:g])
            sub = W["w_in"][l][k0 * 128:(k0 + kgs[g]) * 128][:, pieces[pi]]
            return sub.reshape(kgs[g], 128, -1).transpose(1, 0, 2)
        if kind == "pw":
            g = hf * cfg.gh + key[2]
            return W["pool_w"][l][g].reshape(2, 128, 256).transpose(1, 0, 2)
        raise KeyError(key)
    return get


def mixer_consts(cfg, hf):
    S, gh, ah = cfg.S, cfg.gh, cfg.ah
    idx = np.arange(128)
    ch = idx // 64
    same = ch[:, None] == ch[None, :]
    L = (same & (idx[:, None] <= idx[None, :])).astype(np.float32)
    ref = ch * 64 + 31
    last = ch * 64 + 63
    M1 = L - L[:, ref]
    M4 = L[:, last] - L
    mBD = L.copy()
    mTri = (idx[:, None] <= idx[None, :]).astype(np.float32)
    consts = np.stack([M1, L, M4, mBD, mTri]).astype(np.float32)
    poolP = np.zeros((gh, 3, 128, 128), np.float32)
    for gi in range(gh):
        w = POOL_WINDOWS[hf * gh + gi]
        for first in (0, 1):
            for t in range(128):
                tg = t if first == 0 else t + 128
                start = max(tg + 1 - w, 0)
                cntv = tg + 1 - start
                for sg in range(start, tg + 1):
                    sl = sg if first == 0 else sg - 128
                    if sl >= 0:
                        poolP[gi, first, sl, t] += 1.0 / cntv
                    else:
                        poolP[gi, 2, sl + 128, t] += 1.0 / cntv
                poolP[gi, first, t, t] -= 1.0
    nH = cfg.DAH
    slopes = 2.0 ** (-8.0 * np.arange(1, nH + 1) / nH)
    pos = np.arange(S)
    alq = np.zeros((ah, 4, S), np.float32)
    alk = np.zeros((ah, 4, S), np.float32)
    for h in range(ah):
        sl = slopes[hf * ah + h]
        i_, b_ = pos % 128, pos // 128
        alq[h] = np.stack([-8 * sl * i_, np.ones(S), -8 * sl * 128 * b_, np.ones(S)])
        alk[h] = np.stack([np.ones(S), 8 * sl * i_, np.ones(S), 8 * sl * 128 * b_])
    return consts, poolP, alq.astype(NP_BF16), alk.astype(NP_BF16)


_CACHE = {}


def _prog(kind, cfg, layer):
    key = (kind, layer, cfg.D, cfg.S, cfg.gather)
    if key not in _CACHE:
        _CACHE[key] = (build_mixer if kind == "M" else build_dense)(cfg, layer)
    return _CACHE[key]


def kernel(x, norm_mix_pre, norm_mix_post, norm_ffn_pre, norm_ffn_post, w_in, hgrn_lb_logits, hgrn_out_norm,
           pool_w, pool_scale, diff_lambda, diff_subln, w_up_a, w_up_b, w_up_c, w_out, w_ffn_gate, w_ffn_up,
           w_ffn_down):
    f32 = lambda a: np.ascontiguousarray(np.asarray(a), dtype=np.float32)
    x = f32(x)
    B, S, D = x.shape
    W = dict(w_in=f32(w_in), pool_w=f32(pool_w), w_up_a=f32(w_up_a), w_up_b=f32(w_up_b), w_up_c=f32(w_up_c),
             w_out=f32(w_out), w_ffn_gate=f32(w_ffn_gate), w_ffn_up=f32(w_ffn_up), w_ffn_down=f32(w_ffn_down))
    depth = W["w_in"].shape[0]
    lbl, hgn, psc = f32(hgrn_lb_logits), f32(hgrn_out_norm), f32(pool_scale)
    lam, sub = f32(diff_lambda), f32(diff_subln)
    gains_all = [f32(norm_mix_pre), f32(norm_mix_post), f32(norm_ffn_pre), f32(norm_ffn_post)]
    n_cores = 2 * B
    cfgM = Cfg(D=D, S=S, B=B, depth=depth, F=W["w_ffn_gate"].shape[2], gather=False)
    cfgD = Cfg(D=D, S=S, B=B, depth=depth, F=W["w_ffn_gate"].shape[2], gather=False)
    T = S // 2
    hh, gh, ah = cfgM.hh, cfgM.gh, cfgM.ah
    mconst = [mixer_consts(cfgM, hf) for hf in range(2)]
    cur = x
    for l in range(depth):
        ncM, wsM = _prog("M", cfgM, l)
        mstream = [pack_stream(wsM.plan, wsM.total_padded(), mixer_getter(cfgM, W, hf)) for hf in range(2)]
        in_maps = []
        for c in range(n_cores):
            b, hf = c // 2, c % 2
            consts, poolP, alq, alk = mconst[hf]
            in_maps.append(dict(
                x_full=cur[b], gain=gains_all[0][l], lb_logits=np.ascontiguousarray(lbl[:, hf * hh * 128:(hf + 1) * hh * 128]),
                hg_gain=np.ascontiguousarray(hgn[l, hf * hh * 128:(hf + 1) * hh * 128]),
                pool_scale=np.ascontiguousarray(psc[l, hf * gh * 256:(hf + 1) * gh * 256]), poolP=poolP,
                lam_p=np.ascontiguousarray(lam[l].reshape(-1)), subln=sub[l], alibi_q=alq, alibi_k=alk, consts=consts,
                wstream=mstream[hf]))
        res = run_bass_kernel_spmd(ncM, in_maps, core_ids=list(range(n_cores)))
        yT_half = [np.asarray(r["yT_out"]) for r in res.results]
        ncD, wsD = _prog("D", cfgD, l)
        dstream = stream_inputs(cfgD, wsD, dense_getter(cfgD, W), n_cores)
        gl = np.stack([g[l] for g in gains_all])
        in_maps = []
        for c in range(n_cores):
            b, th = c // 2, c % 2
            y0, y1 = yT_half[2 * b], yT_half[2 * b + 1]
            parts = []
            for (o0, n) in ((0, hh), (hh, 2 * gh), (hh + 2 * gh, ah)):
                parts += [y0[o0:o0 + n], y1[o0:o0 + n]]
            yT = np.ascontiguousarray(np.concatenate(parts, 0)[:, :, th * T:(th + 1) * T])
            in_maps.append(dict(x_own=np.ascontiguousarray(cur[b, th * T:(th + 1) * T]), yT=yT, gains=gl, wstream=dstream[c]))
        res = run_bass_kernel_spmd(ncD, in_maps, core_ids=list(range(n_cores)))
        nxt = np.empty_like(cur)
        for c in range(n_cores):
            b, th = c // 2, c % 2
            nxt[b, th * T:(th + 1) * T] = np.asarray(res.results[c]["x_out"])
        cur = nxt
    return cur
```
